# Optimizing a Trainium2 kernel written in Bass

```python
import math
import jax, jax.numpy as jnp
from jax import lax
import numpy as np

D_MODEL = 2048
BATCH = 2
SEQ = 4096
DEPTH = 4
DEC_BATCH = 8
DEC_SEQ = 4
PAST_LEN = 16384
PAGE_SIZE = 128

N_MIXERS = 2
N_LRU = (DEPTH + 1) // 2
N_ATTN = DEPTH // 2
LRU_WIDTH = D_MODEL
LRU_HEADS = 8
LRU_HEAD_DIM = LRU_WIDTH // LRU_HEADS
LRU_CONV = 4
LRU_C = 8.0
ATTN_GROUPS = ((128, 1), (512, 4), (2048, 16))
N_GROUPS = len(ATTN_GROUPS)
GROUP_HEADS = 8
HEAD_DIM = 128
ATTN_WIDTH = GROUP_HEADS * HEAD_DIM
QKV_WIDTH = N_GROUPS * 3 * GROUP_HEADS * HEAD_DIM
D_FF = 3 * D_MODEL
FFN_CONV = 3
EPS = 1e-6
NEG = -1e30

kernel_name = 'hybrid_rglru_dilated_swa_convffn_step'


def rms_norm(x, g):
    xf = x.astype(jnp.float32)
    y = xf * lax.rsqrt(jnp.mean(xf * xf, axis=-1, keepdims=True) + EPS) * g.astype(jnp.float32)
    return y.astype(x.dtype)


def causal_dwconv(ext, w, b):
    K = w.shape[0]
    L = ext.shape[1] - K + 1
    out = b
    for k in range(K):
        out = out + ext[:, k:k + L] * w[k]
    return out


def _lin_comb(left, right):
    a1, b1 = left
    a2, b2 = right
    return a1 * a2, a2 * b1 + b2


def rg_lru(u, h0, w_a, b_a, w_i, b_i, lam):
    B, L, W = u.shape
    uf = u.astype(jnp.float32)
    ub = uf.reshape(B, L, LRU_HEADS, LRU_HEAD_DIM)
    r = jax.nn.sigmoid(jnp.einsum('blhi,hij->blhj', ub, w_a.astype(jnp.float32)).reshape(B, L, W) + b_a.astype(jnp.float32))
    ig = jax.nn.sigmoid(jnp.einsum('blhi,hij->blhj', ub, w_i.astype(jnp.float32)).reshape(B, L, W) + b_i.astype(jnp.float32))
    log_a = -LRU_C * r * jax.nn.softplus(-lam.astype(jnp.float32))
    a = jnp.exp(log_a)
    bx = jnp.sqrt(-jnp.expm1(2.0 * log_a)) * (ig * uf)
    bx = bx.at[:, 0].add(a[:, 0] * h0.astype(jnp.float32))
    _, h = lax.associative_scan(_lin_comb, (a, bx), axis=1)
    return h.astype(u.dtype), h[:, -1].astype(u.dtype)


def recurrent_block(x, h0, conv_buf, w_in, b_in, conv_w, conv_b, w_a, b_a, w_i, b_i, lam, w_out, b_out):
    proj = x @ w_in + b_in
    gate, u = jnp.split(proj, 2, axis=-1)
    u_ext = jnp.concatenate([conv_buf.astype(u.dtype), u], axis=1)
    uc = causal_dwconv(u_ext, conv_w, conv_b)
    h, h_last = rg_lru(uc, h0, w_a, b_a, w_i, b_i, lam)
    y = (h * jax.nn.gelu(gate)) @ w_out + b_out
    return y, h_last, u_ext[:, u_ext.shape[1] - (LRU_CONV - 1):]


def conv_ffn(x, conv_buf, w_up, conv_w, conv_b, w_down):
    up = x @ w_up
    up_ext = jnp.concatenate([conv_buf.astype(up.dtype), up], axis=1)
    c = causal_dwconv(up_ext, conv_w, conv_b)
    g, v = jnp.split(c, 2, axis=-1)
    y = (jax.nn.gelu(g) * v) @ w_down
    return y, up_ext[:, up_ext.shape[1] - (FFN_CONV - 1):]


def alibi_slopes():
    n = N_GROUPS * GROUP_HEADS
    s = 2.0 ** (-8.0 * np.arange(1, n + 1) / n)
    return jnp.asarray(s, dtype=jnp.float32).reshape(N_GROUPS, GROUP_HEADS)


def dilated_group_prompt(q, k, v, window, dil, slopes):
    B, S, H, Dh = q.shape
    blk = window // dil
    L = -(-S // dil)
    L = -(-L // blk) * blk
    nb = L // blk
    pad = L * dil - S

    def phase(t):
        t = jnp.pad(t, ((0, 0), (0, pad), (0, 0), (0, 0)))
        t = t.reshape(B, L, dil, H, Dh).transpose(0, 2, 1, 3, 4)
        return t.reshape(B, dil, nb, blk, H, Dh)

    def with_prev(t):
        prev = jnp.pad(t, ((0, 0), (0, 0), (1, 0), (0, 0), (0, 0), (0, 0)))[:, :, :-1]
        return jnp.concatenate([prev, t], axis=3)

    qb = phase(q)
    kk = with_prev(phase(k))
    vv = with_prev(phase(v))
    s = jnp.einsum('bpnihd,bpnjhd->bpnhij', qb, kk).astype(jnp.float32) / math.sqrt(Dh)
    i = jnp.arange(blk)[:, None]
    j = jnp.arange(2 * blk)[None, :]
    steps = i + blk - j
    n = jnp.arange(nb)[:, None, None]
    valid = (steps >= 0) & (steps <= blk) & ((n > 0) | (j >= blk))
    bias = -slopes[:, None, None] * (steps * dil).astype(jnp.float32)
    s = jnp.where(valid[None, None, :, None], s + bias, NEG)
    m = jnp.max(s, axis=-1, keepdims=True)
    e = jnp.exp(s - m)
    den = jnp.sum(e, axis=-1)
    lse = m[..., 0] + jnp.log(den)
    o = jnp.einsum('bpnhij,bpnjhd->bpnihd', e, vv.astype(jnp.float32)) / jnp.swapaxes(den, 3, 4)[..., None]
    o = o.reshape(B, dil, L, H, Dh).transpose(0, 2, 1, 3, 4).reshape(B, L * dil, H, Dh)[:, :S]
    lse = jnp.swapaxes(lse, 3, 4).reshape(B, dil, L, H).transpose(0, 2, 1, 3).reshape(B, L * dil, H)[:, :S]
    return o, lse


def dilated_group_sample(q, k_all, v_all, window, dil, slopes):
    B, T, H, Dh = q.shape
    wb = k_all.shape[1] - T
    steps = jnp.arange(window // dil + 1)
    idx = wb + jnp.arange(T)[:, None] - steps[None, :] * dil
    valid = idx >= 0
    idxc = jnp.maximum(idx, 0)
    kg = k_all[:, idxc]
    vg = v_all[:, idxc]
    dist = (steps * dil).astype(jnp.float32)
    s = jnp.einsum('bthd,btkhd->bthk', q, kg).astype(jnp.float32) / math.sqrt(Dh) - slopes[:, None] * dist[None, :]
    s = jnp.where(valid[:, None, :], s, NEG)
    m = jnp.max(s, axis=-1, keepdims=True)
    e = jnp.exp(s - m)
    den = jnp.sum(e, axis=-1)
    lse = m[..., 0] + jnp.log(den)
    o = jnp.einsum('bthk,btkhd->bthd', e, vg.astype(jnp.float32)) / den[..., None]
    return o, lse


def attention_block(x, kv_bufs, w_qkv, w_o):
    B, L, _ = x.shape
    qkv = (x @ w_qkv).reshape(B, L, N_GROUPS, 3, GROUP_HEADS, HEAD_DIM)
    slopes = alibi_slopes()
    outs, lses, new_kv = [], [], []
    for g, (win, dil) in enumerate(ATTN_GROUPS):
        q, k, v = qkv[:, :, g, 0], qkv[:, :, g, 1], qkv[:, :, g, 2]
        if kv_bufs is None:
            o, lse = dilated_group_prompt(q, k, v, win, dil, slopes[g])
            keep = min(win, L)
            new_kv.append(jnp.stack([k[:, L - keep:], v[:, L - keep:]], axis=2))
        else:
            buf = kv_bufs[g].astype(k.dtype)
            k_all = jnp.concatenate([buf[:, :, 0], k], axis=1)
            v_all = jnp.concatenate([buf[:, :, 1], v], axis=1)
            o, lse = dilated_group_sample(q, k_all, v_all, win, dil, slopes[g])
            new_kv.append(jnp.stack([k, v], axis=2))
        outs.append(o)
        lses.append(lse)
    wgt = jax.nn.softmax(jnp.stack(lses, axis=0), axis=0)
    o = jnp.sum(jnp.stack(outs, axis=0) * wgt[..., None], axis=0)
    y = o.reshape(B, L, ATTN_WIDTH).astype(x.dtype) @ w_o
    return y, new_kv


def run_trunk(x, lru_h, lru_conv, kv_caches, ffn_conv, norm_mix, norm_ffn, norm_final, lru_p, attn_p, ffn_p):
    new_h, new_lconv, new_fconv = [], [], []
    new_kv = [[] for _ in ATTN_GROUPS]
    for layer in range(DEPTH):
        j = layer // N_MIXERS
        xn = rms_norm(x, norm_mix[layer])
        if layer % N_MIXERS == 0:
            y, h_last, c_rows = recurrent_block(xn, lru_h[j], lru_conv[j], *[p[j] for p in lru_p])
            new_h.append(h_last)
            new_lconv.append(c_rows)
        else:
            bufs = None if kv_caches is None else [c[j] for c in kv_caches]
            y, kv_rows = attention_block(xn, bufs, attn_p[0][j], attn_p[1][j])
            for g in range(N_GROUPS):
                new_kv[g].append(kv_rows[g])
        x = x + y
        y, f_rows = conv_ffn(rms_norm(x, norm_ffn[layer]), ffn_conv[layer], *[p[layer] for p in ffn_p])
        new_fconv.append(f_rows)
        x = x + y
    out = rms_norm(x, norm_final)
    kv_out = [jnp.stack(rows, axis=0) for rows in new_kv]
    return out, jnp.stack(new_h, 0), jnp.stack(new_lconv, 0), kv_out, jnp.stack(new_fconv, 0)


def setup_inputs(seed: int = 0) -> dict:
    key = jax.random.key(seed)
    ks = jax.random.split(key, 32)
    f32 = jnp.float32
    nrm = lambda k, shape, scale: jax.random.normal(k, shape, f32) * scale
    a_c = jax.random.uniform(ks[0], (N_LRU, LRU_WIDTH), f32, minval=0.9, maxval=0.999)
    a_base = a_c ** (1.0 / LRU_C)
    lam = jnp.log(a_base) - jnp.log1p(-a_base)
    wbufs = [min(w, PAST_LEN) for (w, _) in ATTN_GROUPS]
    return {
        'x_prompt': nrm(ks[1], (BATCH, SEQ, D_MODEL), 1.0),
        'x_sample': nrm(ks[2], (DEC_BATCH, DEC_SEQ, D_MODEL), 1.0),
        'cache_kv_w128': nrm(ks[3], (N_ATTN, DEC_BATCH, wbufs[0], 2, GROUP_HEADS, HEAD_DIM), 1.0),
        'cache_kv_w512': nrm(ks[4], (N_ATTN, DEC_BATCH, wbufs[1], 2, GROUP_HEADS, HEAD_DIM), 1.0),
        'cache_kv_w2048': nrm(ks[5], (N_ATTN, DEC_BATCH, wbufs[2], 2, GROUP_HEADS, HEAD_DIM), 1.0),
        'state_lru_h': nrm(ks[6], (N_LRU, DEC_BATCH, LRU_WIDTH), 0.5),
        'state_lru_conv': nrm(ks[7], (N_LRU, DEC_BATCH, LRU_CONV - 1, LRU_WIDTH), 1.0),
        'state_ffn_conv': nrm(ks[8], (DEPTH, DEC_BATCH, FFN_CONV - 1, 2 * D_FF), 1.0),
        'norm_mix': 1.0 + nrm(ks[9], (DEPTH, D_MODEL), 0.01),
        'norm_ffn': 1.0 + nrm(ks[10], (DEPTH, D_MODEL), 0.01),
        'norm_final': 1.0 + nrm(ks[11], (D_MODEL,), 0.01),
        'lru_w_in': nrm(ks[12], (N_LRU, D_MODEL, 2 * LRU_WIDTH), D_MODEL ** -0.5),
        'lru_b_in': nrm(ks[13], (N_LRU, 2 * LRU_WIDTH), 0.01),
        'lru_conv_w': nrm(ks[14], (N_LRU, LRU_CONV, LRU_WIDTH), LRU_CONV ** -0.5),
        'lru_conv_b': nrm(ks[15], (N_LRU, LRU_WIDTH), 0.01),
        'lru_w_a': nrm(ks[16], (N_LRU, LRU_HEADS, LRU_HEAD_DIM, LRU_HEAD_DIM), LRU_HEAD_DIM ** -0.5),
        'lru_b_a': nrm(ks[17], (N_LRU, LRU_WIDTH), 0.01),
        'lru_w_i': nrm(ks[18], (N_LRU, LRU_HEADS, LRU_HEAD_DIM, LRU_HEAD_DIM), LRU_HEAD_DIM ** -0.5),
        'lru_b_i': nrm(ks[19], (N_LRU, LRU_WIDTH), 0.01),
        'lru_lambda': lam,
        'lru_w_out': nrm(ks[20], (N_LRU, LRU_WIDTH, D_MODEL), LRU_WIDTH ** -0.5),
        'lru_b_out': nrm(ks[21], (N_LRU, D_MODEL), 0.01),
        'attn_w_qkv': nrm(ks[22], (N_ATTN, D_MODEL, QKV_WIDTH), D_MODEL ** -0.5),
        'attn_w_o': nrm(ks[23], (N_ATTN, ATTN_WIDTH, D_MODEL), ATTN_WIDTH ** -0.5),
        'ffn_w_up': nrm(ks[24], (DEPTH, D_MODEL, 2 * D_FF), D_MODEL ** -0.5),
        'ffn_conv_w': nrm(ks[25], (DEPTH, FFN_CONV, 2 * D_FF), FFN_CONV ** -0.5),
        'ffn_conv_b': nrm(ks[26], (DEPTH, 2 * D_FF), 0.01),
        'ffn_w_down': nrm(ks[27], (DEPTH, D_FF, D_MODEL), D_FF ** -0.5),
    }


def reference(x_prompt, x_sample, cache_kv_w128, cache_kv_w512, cache_kv_w2048, state_lru_h, state_lru_conv, state_ffn_conv,
              norm_mix, norm_ffn, norm_final, lru_w_in, lru_b_in, lru_conv_w, lru_conv_b, lru_w_a, lru_b_a, lru_w_i, lru_b_i,
              lru_lambda, lru_w_out, lru_b_out, attn_w_qkv, attn_w_o, ffn_w_up, ffn_conv_w, ffn_conv_b, ffn_w_down):
    lru_p = (lru_w_in, lru_b_in, lru_conv_w, lru_conv_b, lru_w_a, lru_b_a, lru_w_i, lru_b_i, lru_lambda, lru_w_out, lru_b_out)
    attn_p = (attn_w_qkv, attn_w_o)
    ffn_p = (ffn_w_up, ffn_conv_w, ffn_conv_b, ffn_w_down)
    dt = x_prompt.dtype
    Bp = x_prompt.shape[0]
    y_p, h_p, lc_p, kv_p, fc_p = run_trunk(
        x_prompt,
        jnp.zeros((N_LRU, Bp, LRU_WIDTH), dt),
        jnp.zeros((N_LRU, Bp, LRU_CONV - 1, LRU_WIDTH), dt),
        None,
        jnp.zeros((DEPTH, Bp, FFN_CONV - 1, 2 * D_FF), dt),
        norm_mix, norm_ffn, norm_final, lru_p, attn_p, ffn_p)
    y_s, h_s, lc_s, kv_s, fc_s = run_trunk(
        x_sample, state_lru_h, state_lru_conv, (cache_kv_w128, cache_kv_w512, cache_kv_w2048), state_ffn_conv,
        norm_mix, norm_ffn, norm_final, lru_p, attn_p, ffn_p)
    return (y_p, y_s, kv_p[0], kv_p[1], kv_p[2], h_p, lc_p, fc_p, kv_s[0], kv_s[1], kv_s[2], h_s, lc_s, fc_s)
```

```python
import math
import os
import numpy as np
import concourse.bass as bass
import concourse.mybir as mybir
from concourse.bass_utils import run_bass_kernel_spmd

F32 = mybir.dt.float32
BF16 = mybir.dt.bfloat16
AF = mybir.ActivationFunctionType
ALU = mybir.AluOpType

GROUPS = ((128, 1), (512, 4), (2048, 16))
NH = 8
DH = 128
EPS = 1e-6
LRU_C = 8.0


class Cfg:
    def __init__(self, D=2048, SEQ=4096, NT=512, DEPTH=4, NS=4, do_sample=True, NP=1, NSB=1, do_prompt=True):
        self.D = D
        self.SEQ = SEQ
        self.NT = NT
        self.DEPTH = DEPTH
        self.NS = NS
        self.DC = D // 128
        self.DFF = 3 * D
        self.FC = self.DFF // 128
        self.G = 12 if self.FC % 12 == 0 else 8
        self.NPASS = self.FC // self.G
        self.NLRU = (DEPTH + 1) // 2
        self.NATT = DEPTH // 2
        self.CH = D // 8 // 128
        self.QKV = 3 * 3 * NH * DH
        self.do_sample = do_sample
        self.NP = NP
        self.do_prompt = do_prompt
        self.NSB = NSB


SEM_LIMIT = int(os.environ.get("KSEM", "30000"))


class Tok:
    __slots__ = ("eng", "val", "sem", "key")

    def __init__(self, eng, val):
        self.eng = eng
        self.val = val
        self.sem = eng.sem
        self.key = eng.name + "#%d" % eng.epoch


class Res:
    __slots__ = ("name", "w", "r")

    def __init__(self, name):
        self.name = name
        self.w = None
        self.r = {}


class Eng:
    def __init__(self, nc, e, name, inorder=False):
        self.nc = nc
        self.e = e
        self.name = name
        self.epoch = 0
        self.sem = nc.alloc_semaphore(name="s_%s_0" % name)
        self.count = 0
        self.total = 0
        self.waited = {}
        self.inorder = inorder

    def wait_tok(self, d):
        if d is None:
            return
        if d.eng is self and self.inorder:
            return
        if self.waited.get(d.key, 0) < d.val:
            self.e.wait_ge(d.sem, d.val)
            self.waited[d.key] = d.val

    def pre(self, reads=(), writes=()):
        for r in reads:
            self.wait_tok(r.w)
        for w in writes:
            self.wait_tok(w.w)
            for t in w.r.values():
                self.wait_tok(t)

    def post(self, ins, reads=(), writes=()):
        if self.count >= SEM_LIMIT:
            self.epoch += 1
            self.sem = self.nc.alloc_semaphore(name="s_%s_%d" % (self.name, self.epoch))
            self.count = 0
        self.count += 1
        self.total += 1
        ins.then_inc(self.sem, 1)
        t = Tok(self, self.count)
        for r in reads:
            r.r[self.name] = t
        for w in writes:
            w.w = t
            w.r = {}
        return t


class DmaChan:
    def __init__(self, nc, idx):
        self.nc = nc
        self.name = "dma%d" % idx
        self.epoch = 0
        self.sem = nc.alloc_semaphore(name="s_dma%d_0" % idx)
        self.count = 0
        self.inorder = False
        self.last = None


class Ctx:
    pass


import os
SKIP = os.environ.get("KSKIP", "")
KATT = os.environ.get("KATT", "")
FIRSTALL = os.environ.get("KFIRST", "1") == "1"


def build(cfg, n_tiles=None):
    nc = bass.Bass("TRN2", target_bir_lowering=False)
    D, DC, NT, SEQ, DEPTH = cfg.D, cfg.DC, cfg.NT, cfg.SEQ, cfg.DEPTH
    FC, G, NPASS, CH, NS = cfg.FC, cfg.G, cfg.NPASS, cfg.CH, cfg.NS
    NLRU, NATT = cfg.NLRU, cfg.NATT
    NTILES = SEQ // NT if n_tiles is None else n_tiles

    def din(name, shape, dt=F32):
        return nc.dram_tensor(name, list(shape), dt, kind="ExternalInput").ap()

    def dout(name, shape):
        return nc.dram_tensor(name, list(shape), F32, kind="ExternalOutput").ap()

    NP, NSB = cfg.NP, cfg.NSB
    xp_f = din("xp", [NP, SEQ, D])
    xs_f = din("xs", [NSB, NS, D])
    caches_f = [din("cache%d" % g, [NSB, NATT, GROUPS[g][0], 2, NH, DH]) for g in range(3)]
    st_h_f = din("st_h", [NSB, NLRU, DC, 128])
    st_conv_f = din("st_conv", [NSB, NLRU, 3, DC, 128])
    st_ffn_f = din("st_ffn", [NSB, DEPTH, 2, 2 * FC, 128])
    nrm = din("nrm", [128, 2 * DEPTH + 1, DC])
    lru_small = din("lru_small", [128, NLRU, 10, DC])
    lru_bout = din("lru_bout", [128, NLRU, DC])
    ffn_small = din("ffn_small", [128, DEPTH, 4, 2 * FC])
    etab = din("etab", [128, 3, NH, 2, 128])
    ecur16 = din("ecur16", [128, NH, 128])
    ident_d = din("ident", [128, 128])
    w_in = [din("w_in%d" % j, [2 * DC, 128, DC * 128]) for j in range(NLRU)]
    w_gate = din("w_gate", [NLRU, 2, DC, 128, CH * 128])
    w_out = din("w_out", [NLRU, DC, 128, DC * 128])
    w_qkv = [din("w_qkv%d" % a, [72, 128, DC * 128]) for a in range(NATT)]
    w_o = din("w_o", [NATT, DC, 128, NH * 128])
    w_up = [din("w_up%d" % l, [2 * FC, 128, DC * 128]) for l in range(DEPTH)]
    w_down = [din("w_down%d" % l, [NPASS, DC, 128, G * 128]) for l in range(DEPTH)]

    yp_f = dout("yp", [NP, SEQ, D])
    ys_f = dout("ys", [NSB, NS, D])
    kvp_f = [dout("kvp%d" % g, [NP, NATT, min(GROUPS[g][0], SEQ), 2, NH, DH]) for g in range(3)]
    hp_f = dout("hp", [NP, NLRU, DC, 128])
    lcp_f = dout("lcp", [NP, NLRU, 3, DC, 128])
    fcp_f = dout("fcp", [NP, DEPTH, 2, 2 * FC, 128])
    kvs_f = [dout("kvs%d" % g, [NSB, NATT, NS, 2, NH, DH]) for g in range(3)]
    hs_f = dout("hs", [NSB, NLRU, DC, 128])
    lcs_f = dout("lcs", [NSB, NLRU, 3, DC, 128])
    fcs_f = dout("fcs", [NSB, DEPTH, 2, 2 * FC, 128])
    caches = [None] * 3
    kvp = [None] * 3
    kvs = [None] * 3
    kvscr = nc.dram_tensor("kvscr", [NATT, 3, SEQ, 2, NH, DH], F32, kind=("ExternalOutput" if os.environ.get("KSCR", "") == "out" else "Internal")).ap()

    PE = Eng(nc, nc.tensor, "pe", inorder=True)
    ACT = Eng(nc, nc.scalar, "act")
    DVE = Eng(nc, nc.vector, "dve")
    SP = nc.sync
    POOL = nc.gpsimd
    NCH = 24
    chans = [DmaChan(nc, i) for i in range(NCH)]
    chan_rr = [0]
    sp_waited = {}
    pool_waited = {}
    out_dma_toks = []

    def q_wait(q, waited, d):
        if d is None:
            return
        if waited.get(d.key, 0) < d.val:
            q.wait_ge(d.sem, d.val)
            waited[d.key] = d.val

    def dma(out, in_, reads=(), writes=(), is_out=False, chan=None, queue=None, waited=None):
        q = SP if queue is None else queue
        wd = sp_waited if waited is None else waited
        if chan is None:
            chan = chans[chan_rr[0] % NCH]
            chan_rr[0] += 1
        for r in reads:
            q_wait(q, wd, r.w)
        for w in writes:
            q_wait(q, wd, w.w)
            for t in w.r.values():
                q_wait(q, wd, t)
        q_wait(q, wd, chan.last)
        if chan.count >= SEM_LIMIT:
            chan.epoch += 1
            chan.sem = nc.alloc_semaphore(name="s_%s_%d" % (chan.name, chan.epoch))
            chan.count = 0
        ins = q.dma_start(out=out, in_=in_)
        chan.count += 16
        ins.then_inc(chan.sem, 16)
        t = Tok(chan, chan.count)
        chan.last = t
        for r in reads:
            r.r[chan.name] = t
        for w in writes:
            w.w = t
            w.r = {}
        if is_out:
            out_dma_toks.append(t)
        return t

    def X(eng, fn, reads=(), writes=()):
        eng.pre(reads, writes)
        ins = fn()
        return eng.post(ins, reads, writes)

    def sb(name, shape, dt=F32):
        return nc.alloc_sbuf_tensor(name, list(shape), dt)

    ident = sb("ident_s", [128, 128])
    ident_b = sb("ident_b", [128, 128], BF16)
    ones_b = sb("ones_b", [128, 128], BF16)
    nrm_s = sb("nrm_s", [128, 2 * DEPTH + 1, DC])
    lru_s = sb("lru_s", [128, NLRU, 10, DC])
    lru_bo = sb("lru_bo", [128, NLRU, DC])
    nsl = sb("nsl", [128, NLRU, DC])
    ffn_s = sb("ffn_s", [128, DEPTH, 4, 2 * FC])
    eps_t = sb("eps_t", [128, 1])
    e_s = sb("e_s", [128, NH, 2, 128])
    e16 = sb("e16", [128, NH, 128])
    xT = sb("xT", [128, DC, NT])
    xn = sb("xn", [128, DC, NT], BF16)
    rstd = sb("rstd", [128, NT])
    STG = max(2 * D, 4 * NH * DH)
    stage = sb("stage", [128, STG])
    WB = 6
    wring = [sb("wr%d" % i, [128, 2048], BF16) for i in range(WB)]
    hcar = [sb("hcar%d" % k, [128, NLRU, DC]) for k in range(2)]
    ucar = [sb("ucar%d" % k, [128, NLRU, DC, 3]) for k in range(2)]
    fcar = [sb("fcar%d" % k, [128, DEPTH, 2 * FC, 2]) for k in range(2)]
    fmT = [sb("fmT%d" % i, [128, NT]) for i in range(2)]
    hg = sb("hg", [128, DC, NT], BF16)
    khv = [sb("khv%d" % i, [128, 2, DH]) for i in range(2)]
    khT = [sb("khT%d" % i, [128, 128], BF16) for i in range(2)]
    vhb = [sb("vhb%d" % i, [128, DH], BF16) for i in range(2)]
    p_f = [sb("p_f%d" % i, [128, 128]) for i in range(2)]
    p_b = [sb("p_b%d" % i, [128, 128], BF16) for i in range(2)]
    NSUBA = max(NT // 128, 1)
    att_words = (NH * NT) // 2 * 2 + (NSUBA * NH * DH) // 2 + 2 * NH * NT
    lru_words = (CH * NT) // 2 * 2 + CH * (NT + 4) + CH * NT + 5 * NT
    ffn_words = 3 * NT + (G * NT) // 2
    nrm_words = (DC * NT) // 2
    AW = max(att_words, lru_words, ffn_words, nrm_words) + 64
    arena = sb("arena", [128, AW])
    _off = [0]

    def carve(words, dt, pattern=None, **kw):
        a = arena[:, _off[0]:_off[0] + words]
        _off[0] += words
        if dt is BF16:
            a = a.bitcast(BF16)
        if pattern is not None:
            a = a.rearrange(pattern, **kw)
        return a

    _off[0] = 0
    sq = carve((DC * NT) // 2, BF16, "q (c t) -> q c t", c=DC)
    _off[0] = 0
    gg = carve((CH * NT) // 2, BF16, "q (c t) -> q c t", c=CH)
    ucb = carve((CH * NT) // 2, BF16, "q (c t) -> q c t", c=CH)
    uext = carve(CH * (NT + 4), F32, "q (c t) -> q c t", c=CH)
    uc = carve(CH * NT, F32, "q (c t) -> q c t", c=CH)
    t_r = carve(NT, F32)
    t_i = carve(NT, F32)
    t_a = carve(NT, F32)
    t_s = carve(NT, F32)
    t_h = carve(NT, F32)
    _off[0] = 0
    qT = carve((NH * NT) // 2, BF16, "q (h t) -> q h t", h=NH)
    kT = carve((NH * NT) // 2, BF16, "q (h t) -> q h t", h=NH)
    vcur = carve((NSUBA * NH * DH) // 2, BF16, "q (s h e) -> q s h e", s=NSUBA, h=NH)
    accn = carve(NH * NT, F32, "q (h t) -> q h t", h=NH)
    accd = carve(NH * NT, F32, "q (h t) -> q h t", h=NH)
    _off[0] = 0
    cvg = carve(NT, F32)
    cvv = carve(NT, F32)
    hgel = carve(NT, F32)
    hpart = carve((G * NT) // 2, BF16, "q (g t) -> q g t", g=G)

    ps_mm = [nc.alloc_psum_tensor("ps_mm%d" % i, [128, 512], F32) for i in range(2)]
    ps_tr = [nc.alloc_psum_tensor("ps_tr%d" % i, [128, 512], F32) for i in range(2)]
    ps_s = [nc.alloc_psum_tensor("ps_s%d" % i, [128, 512], F32) for i in range(2)]
    ps_num = nc.alloc_psum_tensor("ps_num", [128, 512], F32)
    ps_den = nc.alloc_psum_tensor("ps_den", [128, 512], F32)

    R = {}

    def res(name):
        if name not in R:
            R[name] = Res(name)
        return R[name]

    rr = {"mm": 0, "tr": 0, "s": 0, "w": 0, "kh": 0, "p": 0, "ev": 0, "fm": 0}
    wchan = [DmaChan(nc, 100 + i) for i in range(WB)]

    def wtile(dram_ap, ncols):
        b = rr["w"] % WB
        rr["w"] += 1
        rs = res("wring%d" % b)
        dma(wring[b][:, 0:ncols], dram_ap, reads=(), writes=(rs,), chan=wchan[b], queue=POOL, waited=pool_waited)
        return wring[b], rs

    def precast(src_ap, dst_ap, ncols):
        b = rr["w"] % WB
        rr["w"] += 1
        rs = res("wring%d" % b)
        dma(wring[b][:, 0:ncols], src_ap, reads=(), writes=(rs,), chan=wchan[b], queue=POOL, waited=pool_waited)
        dma(dst_ap, wring[b][:, 0:ncols], reads=(rs,), writes=())

    def mm_bank():
        b = rr["mm"] % 2
        rr["mm"] += 1
        return ps_mm[b], res("ps_mm%d" % b)

    def tr_bank():
        b = rr["tr"] % 2
        rr["tr"] += 1
        return ps_tr[b], res("ps_tr%d" % b)

    def s_bank():
        b = rr["s"] % 2
        rr["s"] += 1
        return ps_s[b], res("ps_s%d" % b)

    def mm_group(out_ap, pairs, reads, out_res, first=True):
        PE.pre(reads, (out_res,))
        n = len(pairs)
        ins = None
        for i, (l, r) in enumerate(pairs):
            ins = nc.tensor.matmul(out_ap, l, r, start=(first and i == 0), stop=(i == n - 1), skip_group_check=True)
        return PE.post(ins, reads, (out_res,))

    def ev_eng():
        rr["ev"] += 1
        return ACT if rr["ev"] % 2 == 0 else DVE

    def copy_on(eng, out, in_, reads, writes, scale=None):
        if eng is ACT:
            if scale is None:
                return X(ACT, lambda: nc.scalar.copy(out=out, in_=in_), reads, writes)
            return X(ACT, lambda: nc.scalar.activation(out=out, in_=in_, func=AF.Copy, scale=scale), reads, writes)
        if scale is None:
            return X(DVE, lambda: nc.vector.tensor_copy(out=out, in_=in_), reads, writes)
        return X(DVE, lambda: nc.vector.tensor_scalar(out=out, in0=in_, scalar1=scale, scalar2=None, op0=ALU.mult), reads, writes)

    r_const = res("const")
    dma(ident[:], ident_d, writes=(r_const,))
    dma(nrm_s[:], nrm, writes=(res("c1"),))
    dma(lru_s[:], lru_small, writes=(res("c2"),))
    dma(lru_bo[:], lru_bout, writes=(res("c3"),))
    dma(ffn_s[:], ffn_small, writes=(res("c4"),))
    dma(e16[:], ecur16, writes=(res("c5"),))
    cres = [res("c%d" % i) for i in range(1, 6)] + [r_const]
    X(DVE, lambda: nc.vector.tensor_copy(out=ident_b[:], in_=ident[:]), cres, (res("identb"),))
    X(DVE, lambda: nc.vector.memset(ones_b[:], 1.0), (), (res("ones"),))
    X(DVE, lambda: nc.vector.memset(eps_t[:], EPS), (), (res("eps"),))
    for j in range(NLRU):
        X(ACT, lambda: nc.scalar.activation(out=nsl[:, j, :], in_=lru_s[:, j, 9, :], func=AF.Exp, scale=-1.0), cres, (res("nsl"),))
        X(ACT, lambda: nc.scalar.activation(out=nsl[:, j, :], in_=nsl[:, j, :], func=AF.Ln, bias=1.0), (res("nsl"),), (res("nsl"),))
        X(DVE, lambda: nc.vector.tensor_scalar(out=nsl[:, j, :], in0=nsl[:, j, :], scalar1=-LRU_C, scalar2=None, op0=ALU.mult), (res("nsl"),), (res("nsl"),))
    CONSTS = cres + [res("identb"), res("ones"), res("eps"), res("nsl")]
    for i_ in range(2):
        X(DVE, lambda: nc.vector.memset(khv[i_][:], 0.0), (), (res("khv%d" % i_),))
    X(DVE, lambda: nc.vector.memset(hcar[0][:], 0.0), (), (res("hcar0"),))
    X(DVE, lambda: nc.vector.memset(ucar[0][:], 0.0), (), (res("ucar0"),))
    X(DVE, lambda: nc.vector.memset(fcar[0][:], 0.0), (), (res("fcar0"),))

    r_xT = res("xT")
    r_xn = res("xn")
    r_sq = res("sq")
    r_rstd = res("rstd")
    r_stage = res("stage")

    def fence():
        engs = (PE, ACT, DVE)
        toks = [Tok(e, e.count) for e in engs if e.count > 0]
        for e in engs:
            for t in toks:
                if t.eng is not e:
                    e.wait_tok(t)

    def rmsnorm(n, gi, fp32_out=None):
        fence()
        for c in range(DC):
            X(ACT, lambda: nc.scalar.activation(out=sq[:, c, 0:n], in_=xT[:, c, 0:n], func=AF.Square), (r_xT,), (r_sq,))
        bank, br = mm_bank()
        mm_group(bank[:, 0:n], [(ones_b[:], sq[:, c, 0:n]) for c in range(DC)], (r_sq, res("ones")), br)
        X(ACT, lambda: nc.scalar.activation(out=rstd[:, 0:n], in_=bank[:, 0:n], func=AF.Sqrt, bias=eps_t[:], scale=1.0 / D), (br, res("eps")), (r_rstd,))
        X(DVE, lambda: nc.vector.reciprocal(out=rstd[:, 0:n], in_=rstd[:, 0:n]), (r_rstd,), (r_rstd,))
        for c in range(DC):
            if fp32_out is None:
                X(DVE, lambda: nc.vector.scalar_tensor_tensor(out=xn[:, c, 0:n], in0=xT[:, c, 0:n], scalar=nrm_s[:, gi, c:c + 1], in1=rstd[:, 0:n], op0=ALU.mult, op1=ALU.mult), (r_xT, r_rstd) + tuple(CONSTS), (r_xn,))
            else:
                o_ap, o_res = fp32_out(c)
                X(DVE, lambda: nc.vector.scalar_tensor_tensor(out=o_ap, in0=xT[:, c, 0:n], scalar=nrm_s[:, gi, c:c + 1], in1=rstd[:, 0:n], op0=ALU.mult, op1=ALU.mult), (r_xT, r_rstd) + tuple(CONSTS), (o_res,))

    def proj_fm(w_ap_tile, kc_n, rhs_fn, n, rhs_res):
        wt, wr = wtile(w_ap_tile, kc_n * 128)
        bank, br = mm_bank()
        mm_group(bank[:, 0:n], [(wt[:, kc * 128:(kc + 1) * 128], rhs_fn(kc)) for kc in range(kc_n)], (wr,) + tuple(rhs_res), br)
        return bank, br

    def residual_add(bank, br, m, n, bias_ap=None):
        if bias_ap is None:
            X(DVE, lambda: nc.vector.tensor_tensor(out=xT[:, m, 0:n], in0=bank[:, 0:n], in1=xT[:, m, 0:n], op=ALU.add), (br, r_xT), (r_xT,))
        else:
            X(DVE, lambda: nc.vector.scalar_tensor_tensor(out=xT[:, m, 0:n], in0=bank[:, 0:n], scalar=bias_ap, in1=xT[:, m, 0:n], op0=ALU.add, op1=ALU.add), (br, r_xT) + tuple(CONSTS), (r_xT,))

    def lru_layer(j, n, k):
        r_h, r_u = res("hcar%d" % k), res("ucar%d" % k)
        r_gg, r_ue, r_uc, r_ucb, r_hg = res("gg"), res("uext"), res("uc"), res("ucb"), res("hg")
        P = lambda i, c: lru_s[:, j, i, c:c + 1]
        for hd in range(8):
            for cc in range(CH):
                c = hd * CH + cc
                bank, br = proj_fm(w_in[j][c], DC, lambda kc: xn[:, kc, 0:n], n, (r_xn,))
                X(ACT, lambda: nc.scalar.activation(out=gg[:, cc, 0:n], in_=bank[:, 0:n], func=AF.Gelu_apprx_tanh, bias=P(0, c)), (br,) + tuple(CONSTS), (r_gg,))
                bank, br = proj_fm(w_in[j][DC + c], DC, lambda kc: xn[:, kc, 0:n], n, (r_xn,))
                X(ACT, lambda: nc.scalar.activation(out=uext[:, cc, 3:3 + n], in_=bank[:, 0:n], func=AF.Identity, bias=P(1, c)), (br,) + tuple(CONSTS), (r_ue,))
                X(DVE, lambda: nc.vector.tensor_copy(out=uext[:, cc, 0:3], in_=ucar[k][:, j, c, :]), (r_u,), (r_ue,))
                X(ACT, lambda: nc.scalar.activation(out=uc[:, cc, 0:n], in_=uext[:, cc, 3:3 + n], func=AF.Identity, bias=P(6, c), scale=P(5, c)), (r_ue,) + tuple(CONSTS), (r_uc,))
                for kk in range(3):
                    X(DVE, lambda: nc.vector.scalar_tensor_tensor(out=uc[:, cc, 0:n], in0=uext[:, cc, kk:kk + n], scalar=P(2 + kk, c), in1=uc[:, cc, 0:n], op0=ALU.mult, op1=ALU.add), (r_ue, r_uc) + tuple(CONSTS), (r_uc,))
                X(ACT, lambda: nc.scalar.copy(out=ucb[:, cc, 0:n], in_=uc[:, cc, 0:n]), (r_uc,), (r_ucb,))
                X(DVE, lambda: nc.vector.tensor_copy(out=ucar[k][:, j, c, :], in_=uext[:, cc, n:n + 3]), (r_ue,), (r_u,))
            for cc in range(CH):
                c = hd * CH + cc
                bank, br = proj_fm(w_gate[j, 0, c], CH, lambda kc: ucb[:, kc, 0:n], n, (r_ucb,))
                X(ACT, lambda: nc.scalar.activation(out=t_r[:, 0:n], in_=bank[:, 0:n], func=AF.Sigmoid, bias=P(7, c)), (br,) + tuple(CONSTS), (res("t_r"),))
                bank, br = proj_fm(w_gate[j, 1, c], CH, lambda kc: ucb[:, kc, 0:n], n, (r_ucb,))
                X(ACT, lambda: nc.scalar.activation(out=t_i[:, 0:n], in_=bank[:, 0:n], func=AF.Sigmoid, bias=P(8, c)), (br,) + tuple(CONSTS), (res("t_i"),))
                X(ACT, lambda: nc.scalar.activation(out=t_a[:, 0:n], in_=t_r[:, 0:n], func=AF.Exp, scale=nsl[:, j, c:c + 1]), (res("t_r"),) + tuple(CONSTS), (res("t_a"),))
                X(DVE, lambda: nc.vector.tensor_tensor(out=t_s[:, 0:n], in0=t_a[:, 0:n], in1=t_a[:, 0:n], op=ALU.mult), (res("t_a"),), (res("t_s"),))
                X(ACT, lambda: nc.scalar.activation(out=t_s[:, 0:n], in_=t_s[:, 0:n], func=AF.Sqrt, bias=1.0, scale=-1.0), (res("t_s"),), (res("t_s"),))
                X(DVE, lambda: nc.vector.tensor_tensor(out=t_i[:, 0:n], in0=t_i[:, 0:n], in1=uc[:, cc, 0:n], op=ALU.mult), (res("t_i"), r_uc), (res("t_i"),))
                X(DVE, lambda: nc.vector.tensor_tensor(out=t_i[:, 0:n], in0=t_i[:, 0:n], in1=t_s[:, 0:n], op=ALU.mult), (res("t_i"), res("t_s")), (res("t_i"),))
                X(DVE, lambda: nc.vector.tensor_tensor_scan(out=t_h[:, 0:n], data0=t_a[:, 0:n], data1=t_i[:, 0:n], initial=hcar[k][:, j, c:c + 1], op0=ALU.mult, op1=ALU.add), (res("t_a"), res("t_i"), r_h), (res("t_h"),))
                X(DVE, lambda: nc.vector.tensor_copy(out=hcar[k][:, j, c:c + 1], in_=t_h[:, n - 1:n]), (res("t_h"),), (r_h,))
                X(DVE, lambda: nc.vector.tensor_tensor(out=hg[:, c, 0:n], in0=t_h[:, 0:n], in1=gg[:, cc, 0:n], op=ALU.mult), (res("t_h"), r_gg), (r_hg,))
        for m in range(DC):
            bank, br = proj_fm(w_out[j, m], DC, lambda kc: hg[:, kc, 0:n], n, (r_hg,))
            residual_add(bank, br, m, n, bias_ap=lru_bo[:, j, m:m + 1])

    def ffn_layer(l, n, k):
        r_f = res("fcar%d" % k)
        r_hp = res("hpart")
        Pf = lambda i, m: ffn_s[:, l, i, m:m + 1]
        for pi in range(NPASS):
            for cc in range(G):
                c = pi * G + cc
                for half, dst, dres in ((0, cvg, res("cvg")), (1, cvv, res("cvv"))):
                    m = half * FC + c
                    bank, br = proj_fm(w_up[l][m], DC, lambda kc: xn[:, kc, 0:n], n, (r_xn,))
                    X(ACT, lambda: nc.scalar.activation(out=dst[:, 0:n], in_=bank[:, 0:n], func=AF.Identity, bias=Pf(3, m), scale=Pf(2, m)), (br,) + tuple(CONSTS), (dres,))
                    X(DVE, lambda: nc.vector.scalar_tensor_tensor(out=dst[:, 1:n], in0=bank[:, 0:n - 1], scalar=Pf(1, m), in1=dst[:, 1:n], op0=ALU.mult, op1=ALU.add), (br, dres) + tuple(CONSTS), (dres,))
                    X(DVE, lambda: nc.vector.scalar_tensor_tensor(out=dst[:, 2:n], in0=bank[:, 0:n - 2], scalar=Pf(0, m), in1=dst[:, 2:n], op0=ALU.mult, op1=ALU.add), (br, dres) + tuple(CONSTS), (dres,))
                    X(DVE, lambda: nc.vector.scalar_tensor_tensor(out=dst[:, 0:1], in0=fcar[k][:, l, m, 1:2], scalar=Pf(1, m), in1=dst[:, 0:1], op0=ALU.mult, op1=ALU.add), (r_f, dres) + tuple(CONSTS), (dres,))
                    X(DVE, lambda: nc.vector.scalar_tensor_tensor(out=dst[:, 0:1], in0=fcar[k][:, l, m, 0:1], scalar=Pf(0, m), in1=dst[:, 0:1], op0=ALU.mult, op1=ALU.add), (r_f, dres) + tuple(CONSTS), (dres,))
                    X(DVE, lambda: nc.vector.scalar_tensor_tensor(out=dst[:, 1:2], in0=fcar[k][:, l, m, 1:2], scalar=Pf(0, m), in1=dst[:, 1:2], op0=ALU.mult, op1=ALU.add), (r_f, dres) + tuple(CONSTS), (dres,))
                    X(DVE, lambda: nc.vector.tensor_copy(out=fcar[k][:, l, m, :], in_=bank[:, n - 2:n]), (br,), (r_f,))
                X(ACT, lambda: nc.scalar.activation(out=hgel[:, 0:n], in_=cvg[:, 0:n], func=AF.Gelu_apprx_tanh), (res("cvg"),), (res("hgel"),))
                X(DVE, lambda: nc.vector.tensor_tensor(out=hpart[:, cc, 0:n], in0=hgel[:, 0:n], in1=cvv[:, 0:n], op=ALU.mult), (res("hgel"), res("cvv")), (r_hp,))
            for m in range(DC):
                bank, br = proj_fm(w_down[l][pi, m], G, lambda kc: hpart[:, kc, 0:n], n, (r_hp,))
                residual_add(bank, br, m, n)

    def attn_layer(a, n, k, T0):
        r_q, r_k, r_v = res("qT"), res("kT"), res("vcur")
        r_an, r_ad, r_o = res("accn"), res("accd"), res("hg")
        r_e = res("e_s")
        nsub = max(n // 128, 1)
        for g, (win, d) in enumerate(GROUPS):
            dma(e_s[:], etab[:, g], writes=(r_e,))
            if k == 0 and d > 1:
                j0m = max(0, 128 - T0 // d)
                if j0m == 96:
                    X(DVE, lambda: nc.vector.memset(e_s[64:96, :, 0, :], 0.0), (r_e,), (r_e,))
                elif j0m == 32:
                    X(DVE, lambda: nc.vector.memset(e_s[0:32, :, 0, :], 0.0), (r_e,), (r_e,))
            for h in range(NH if "q" not in KATT else 0):
                bank, br = proj_fm(w_qkv[a][(g * 3 + 0) * NH + h], DC, lambda kc: xn[:, kc, 0:n], n, (r_xn,))
                copy_on(ev_eng(), qT[:, h, 0:n], bank[:, 0:n], (br,), (r_q,), scale=1.0 / math.sqrt(DH))
            if k == 0:
                def tok_ap(kc, s):
                    if d == 1 or "S" in KATT:
                        return xn[:, kc, s * 128:(s + 1) * 128]
                    v = xn[:, kc, 0:n].rearrange("q (jj ph) -> q ph jj", ph=4)
                    return v[:, s, :]
                rows = 128
            else:
                def tok_ap(kc, s):
                    return xn[:, kc, 0:n]
                rows = n
            kvtok = stage[:, 0:4 * NH * DH].rearrange("q (s h e) -> q s h e", s=4, h=NH)
            r_scr = res("scr%d_%d" % (a, g))
            for t_ in range(2 if "k" not in KATT else 0):
                for h in range(NH):
                    bank, br = proj_fm(w_qkv[a][(g * 3 + 1 + t_) * NH + h], DC, lambda kc: xn[:, kc, 0:n], n, (r_xn,))
                    fi = rr["fm"] % 2
                    rr["fm"] += 1
                    rfm = res("fmT%d" % fi)
                    copy_on(DVE, fmT[fi][:, 0:n], bank[:, 0:n], (br,), (rfm,))
                    if t_ == 0:
                        copy_on(ACT, kT[:, h, 0:n], fmT[fi][:, 0:n], (rfm,), (r_k,))
                    if "x" in KATT:
                            continue
                    tb, tr_ = tr_bank()
                    PE.pre((rfm, r_const), (tr_,))
                    ins = None
                    for s in range(nsub):
                        if k == 1:
                            src = fmT[fi][:, 0:n]
                        elif d == 1 or "S" in KATT:
                            src = fmT[fi][:, s * 128:(s + 1) * 128]
                        else:
                            src = fmT[fi][:, 0:n].rearrange("q (jj ph) -> q ph jj", ph=4)[:, s, :]
                        ins = nc.tensor.transpose(tb[0:rows, s * 128:(s + 1) * 128], src, ident[:, :])
                    PE.post(ins, (rfm, r_const), (tr_,))
                    if "c" in KATT:
                        continue
                    copy_on(DVE, kvtok[0:rows, 0:nsub, h, :], tb[0:rows, 0:nsub * 128].rearrange("q (s e) -> q s e", e=128), (tr_,), (r_stage,))
                    if t_ == 1:
                        copy_on(ACT, vcur[0:rows, 0:nsub, h, :], kvtok[0:rows, 0:nsub, h, :], (r_stage,), (r_v,))
                if k == 0 and "w" in KATT:
                    pass
                elif k == 0:
                    for s in range(nsub):
                        if d == 1:
                            dst = kvscr[a, g, T0 + s * 128:T0 + (s + 1) * 128, t_]
                            dma(dst, kvtok[:, s], reads=(r_stage,), writes=(r_scr,))
                        else:
                            dst = kvscr[a, g, T0:T0 + n, t_].rearrange("(jj ph) h e -> ph jj h e", ph=4)[s]
                            dma(dst, kvtok[:, s], reads=(r_stage,), writes=(r_scr,))
                else:
                    dma(kvs[g][a, :, t_], kvtok[0:n, 0], reads=(r_stage,), writes=(res("kvs_o"),), is_out=True)
            if k == 0:
                keep = min(win, SEQ)
                if T0 + n > SEQ - keep and "o" not in KATT:
                    lo = max(T0, SEQ - keep)
                    dma(kvp[g][a, lo - (SEQ - keep):T0 + n - (SEQ - keep)], kvscr[a, g, lo:T0 + n], reads=(r_scr,), writes=(res("kvp_o"),), is_out=True)
            for h in range(NH if "b" not in KATT else 0):
                blocks = []
                if k == 0:
                    nj = n // d
                    if d == 1:
                        units = [(i, None) for i in range(n // 128)]
                    else:
                        units = [(p, None) for p in range(d)]
                first_bank = [True]

                def score_block(kl_ap, nk, q_ap, nq, e_ap, v_ap, col0, pbase, extra_reads):
                    sbk, sr = s_bank()
                    mm_group(sbk[pbase:pbase + nk, 0:nq], [(kl_ap, q_ap)], (r_q,) + tuple(extra_reads), sr)
                    pi_ = rr["p"] % 2
                    rr["p"] += 1
                    rpf, rpb = res("p_f%d" % pi_), res("p_b%d" % pi_)
                    X(ACT, lambda: nc.scalar.activation(out=p_f[pi_][pbase:pbase + nk, 0:nq], in_=sbk[pbase:pbase + nk, 0:nq], func=AF.Exp), (sr,), (rpf,))
                    X(DVE, lambda: nc.vector.tensor_tensor(out=p_b[pi_][pbase:pbase + nk, 0:nq], in0=p_f[pi_][pbase:pbase + nk, 0:nq], in1=e_ap, op=ALU.mult), (rpf, r_e, res("c5"), res("identb"), r_const), (rpb,))
                    PE.pre((rpb, res("ones")) + tuple(extra_reads) + (r_v,), (res("ps_num"), res("ps_den")))
                    if callable(col0):
                        o_n, o_d = col0(ps_num), col0(ps_den)
                    else:
                        o_n, o_d = ps_num[:, col0:col0 + nq], ps_den[:, col0:col0 + nq]
                    nc.tensor.matmul(o_n, v_ap, p_b[pi_][pbase:pbase + nk, 0:nq], start=first_bank[0], stop=True, skip_group_check=True)
                    ins = nc.tensor.matmul(o_d, ones_b[pbase:pbase + nk, :], p_b[pi_][pbase:pbase + nk, 0:nq], start=first_bank[0], stop=True, skip_group_check=True)
                    PE.post(ins, (rpb, res("ones")) + tuple(extra_reads) + (r_v,), (res("ps_num"), res("ps_den")))
                    first_bank[0] = False

                def load_hist(row_ap_fn, j0, j0e=None):
                    bi = rr["kh"] % 2
                    rr["kh"] += 1
                    rk, rkt, rvb = res("khv%d" % bi), res("khT%d" % bi), res("vhb%d" % bi)
                    if j0e is None:
                        j0e = j0
                    dma(khv[bi][j0:128], row_ap_fn(j0), reads=(r_scr,) if k == 0 else (), writes=(rk,))
                    tb, tr_ = tr_bank()
                    PE.pre((rk, r_const), (tr_,))
                    ins = nc.tensor.transpose(tb[:, j0e:128], khv[bi][j0e:128, 0, :], ident[j0e:128, j0e:128])
                    PE.post(ins, (rk, r_const), (tr_,))
                    copy_on(ev_eng(), khT[bi][:, j0e:128], tb[:, j0e:128], (tr_,), (rkt,))
                    copy_on(ev_eng(), vhb[bi][j0e:128, :], khv[bi][j0e:128, 1, :], (rk,), (rvb,))
                    return bi, (rkt, rvb)

                if k == 0:
                    nj = n // d
                    if d == 1:
                        for i in range(n // 128):
                            q_ap = qT[:, h, i * 128:(i + 1) * 128]
                            if i == 0:
                                if T0 > 0:
                                    bi, rd = load_hist(lambda j0: kvscr[a, g, T0 - 128 + j0:T0, :, h, :], 0)
                                    score_block(khT[bi][:, :], 128, q_ap, 128, e_s[:, h, 0, :], vhb[bi][:, :], i * 128, 0, rd)
                            else:
                                score_block(kT[:, h, (i - 1) * 128:i * 128], 128, q_ap, 128, e_s[:, h, 0, :], vcur[:, i - 1, h, :], i * 128, 0, (r_k,))
                            score_block(kT[:, h, i * 128:(i + 1) * 128], 128, q_ap, 128, e_s[:, h, 1, :], vcur[:, i, h, :], i * 128, 0, (r_k,))
                    else:
                        qv = qT[:, h, 0:n].rearrange("q (jj ph) -> q ph jj", ph=d)
                        kv4 = kT[:, h, 0:n].rearrange("q (jj ph) -> q ph jj", ph=4)
                        qv4 = qT[:, h, 0:n].rearrange("q (jj ph) -> q ph jj", ph=4)
                        for p in range(d):
                            q_ap = qv[:, p, :]
                            if d == 4:
                                col0 = p * nj
                            else:
                                col0 = (lambda t, p=p: t[:, (p % 4) * 128:(p % 4 + 1) * 128].rearrange("q (jj r) -> q r jj", r=4)[:, p // 4, :])
                            j0 = max(0, 128 - T0 // d)
                            if j0 < 128:
                                j0e = 64 if j0 >= 64 else 0
                                if T0 >= 128 * d:
                                    hist_rows = kvscr[a, g, T0 - 128 * d:T0].rearrange("(jj ph) t h e -> ph jj t h e", ph=d)
                                    fn = lambda j0_, p=p, hist_rows=hist_rows: hist_rows[p, j0_:128, :, h, :]
                                else:
                                    hr = kvscr[a, g, 0:T0].rearrange("(jj ph) t h e -> ph jj t h e", ph=d)
                                    fn = lambda j0_, p=p, hr=hr: hr[p, :, :, h, :]
                                bi, rd = load_hist(fn, j0, j0e)
                                score_block(khT[bi][:, j0e:128], 128 - j0e, q_ap, nj, e_s[j0e:128, h, 0, 0:nj], vhb[bi][j0e:128, :], col0, j0e, rd)
                            if d == 4:
                                score_block(kv4[:, p, :], 128, q_ap, 128, e_s[:, h, 1, :], vcur[:, p, h, :], col0, 0, (r_k,))
                        if d == 16:
                            for s_ in range(4):
                                score_block(kv4[:, s_, :], 128, qv4[:, s_, :], 128, e16[:, h, :], vcur[:, s_, h, :], s_ * 128, 0, (r_k,))
                else:
                    wb = win
                    if d == 1:
                        bi, rd = load_hist(lambda j0: caches[g][a, j0:128, :, h, :], 0)
                        score_block(khT[bi][:, :], 128, qT[:, h, 0:n], n, e_s[:, h, 0, 0:n], vhb[bi][:, :], 0, 0, rd)
                        score_block(kT[:, h, 0:n], n, qT[:, h, 0:n], n, e_s[0:n, h, 1, 0:n], vcur[0:n, 0, h, :], 0, 0, (r_k,))
                    else:
                        cv_ = caches[g][a].rearrange("(jj ph) t h e -> ph jj t h e", ph=d)
                        for p in range(n):
                            bi, rd = load_hist(lambda j0, p=p: cv_[p, j0:128, :, h, :], 0)
                            score_block(khT[bi][:, :], 128, qT[:, h, p:p + 1], 1, e_s[:, h, 0, 0:1], vhb[bi][:, :], p, 0, rd)
                        score_block(kT[:, h, 0:n], n, qT[:, h, 0:n], n, ident[0:n, 0:n], vcur[0:n, 0, h, :], 0, 0, (r_k,))
                if k == 0 and d > 1:
                    an = accn[:, h, 0:n].rearrange("q (jj ph) -> q ph jj", ph=4)
                    ad = accd[:, h, 0:n].rearrange("q (jj ph) -> q ph jj", ph=4)
                    pn = ps_num[:, 0:n].rearrange("q (ph jj) -> q ph jj", ph=4)
                    pd = ps_den[:, 0:n].rearrange("q (ph jj) -> q ph jj", ph=4)
                else:
                    an, ad, pn, pd = accn[:, h, 0:n], accd[:, h, 0:n], ps_num[:, 0:n], ps_den[:, 0:n]
                if g == 0:
                    X(ACT, lambda: nc.scalar.copy(out=an, in_=pn), (res("ps_num"),), (r_an,))
                    X(DVE, lambda: nc.vector.tensor_copy(out=ad, in_=pd), (res("ps_den"),), (r_ad,))
                else:
                    X(DVE, lambda: nc.vector.tensor_tensor(out=an, in0=pn, in1=an, op=ALU.add), (res("ps_num"), r_an), (r_an,))
                    X(DVE, lambda: nc.vector.tensor_tensor(out=ad, in0=pd, in1=ad, op=ALU.add), (res("ps_den"), r_ad), (r_ad,))
        for h in range(NH if "f" not in KATT else 0):
            X(DVE, lambda: nc.vector.reciprocal(out=accd[:, h, 0:n], in_=accd[:, h, 0:n]), (r_ad,), (r_ad,))
            X(DVE, lambda: nc.vector.tensor_tensor(out=hg[:, h, 0:n], in0=accn[:, h, 0:n], in1=accd[:, h, 0:n], op=ALU.mult), (r_an, r_ad), (r_o,))
        for m in range(DC if "f" not in KATT else 0):
            bank, br = proj_fm(w_o[a, m], NH, lambda kc: hg[:, kc, 0:n], n, (r_o,))
            residual_add(bank, br, m, n)

    SPH = max(1, STG // D)

    def load_x(src_ap, n):
        nsub = max(n // 128, 1)
        rows = min(n, 128)
        xtok = stage[:, 0:SPH * D].rearrange("q (s f) -> q s f", s=SPH)
        srcv = src_ap.rearrange("(s q) f -> q s f", q=rows)
        for s0 in range(0, nsub, SPH):
            ns = min(SPH, nsub - s0)
            dma(xtok[0:rows, 0:ns, :], srcv[:, s0:s0 + ns, :], reads=(), writes=(r_stage,))
            for c in range(DC):
                tb, tr_ = tr_bank()
                PE.pre((r_stage, r_const), (tr_,))
                ins = None
                for s in range(ns):
                    ins = nc.tensor.transpose(tb[:, s * 128:s * 128 + rows], xtok[0:rows, s, c * 128:(c + 1) * 128], ident[0:rows, 0:rows])
                PE.post(ins, (r_stage, r_const), (tr_,))
                copy_on(ev_eng(), xT[:, c, s0 * 128:s0 * 128 + (ns - 1) * 128 + rows], tb[:, 0:(ns - 1) * 128 + rows], (tr_,), (r_xT,))

    def store_y(dst_ap, n):
        nsub = max(n // 128, 1)
        rows = min(n, 128)
        ytok = stage[:, 0:SPH * D].rearrange("q (s f) -> q s f", s=SPH)
        dstv = dst_ap.rearrange("(s q) f -> q s f", q=rows)
        fence()
        slots = [accn[:, i, :] for i in range(NH)]
        for c in range(DC):
            X(ACT, lambda: nc.scalar.activation(out=sq[:, c, 0:n], in_=xT[:, c, 0:n], func=AF.Square), (r_xT,), (r_sq,))
        bank, br = mm_bank()
        mm_group(bank[:, 0:n], [(ones_b[:], sq[:, c, 0:n]) for c in range(DC)], (r_sq, res("ones")), br)
        X(ACT, lambda: nc.scalar.activation(out=rstd[:, 0:n], in_=bank[:, 0:n], func=AF.Sqrt, bias=eps_t[:], scale=1.0 / D), (br, res("eps")), (r_rstd,))
        X(DVE, lambda: nc.vector.reciprocal(out=rstd[:, 0:n], in_=rstd[:, 0:n]), (r_rstd,), (r_rstd,))
        fence()
        gi = 2 * DEPTH
        for s0 in range(0, nsub, SPH):
            ns = min(SPH, nsub - s0)
            w0 = s0 * 128
            wn = (ns - 1) * 128 + rows
            for c in range(DC):
                o_ap, o_res = slots[c % NH][:, 0:wn], res("yslot%d" % (c % NH))
                X(DVE, lambda: nc.vector.scalar_tensor_tensor(out=o_ap, in0=xT[:, c, w0:w0 + wn], scalar=nrm_s[:, gi, c:c + 1], in1=rstd[:, w0:w0 + wn], op0=ALU.mult, op1=ALU.mult), (r_xT, r_rstd) + tuple(CONSTS), (o_res,))
                tb, tr_ = tr_bank()
                PE.pre((o_res, r_const), (tr_,))
                ins = None
                for s in range(ns):
                    ins = nc.tensor.transpose(tb[0:rows, s * 128:(s + 1) * 128], slots[c % NH][:, s * 128:s * 128 + rows], ident[:, :])
                PE.post(ins, (o_res, r_const), (tr_,))
                copy_on(ev_eng(), ytok[0:rows, 0:ns, c * 128:(c + 1) * 128], tb[0:rows, 0:ns * 128].rearrange("q (s e) -> q s e", e=128), (tr_,), (r_stage,))
            dma(dstv[:, s0:s0 + ns, :], ytok[0:rows, 0:ns, :], reads=(r_stage,), writes=(res("y_o"),), is_out=True)

    def store_fm(src_ap_2d, ncols, dst_rows_ap, src_res):
        c0 = 0
        while c0 < ncols:
            w_ = min(128, ncols - c0)
            tb, tr_ = tr_bank()
            PE.pre((src_res, r_const), (tr_,))
            ins = nc.tensor.transpose(tb[0:w_, 0:128], src_ap_2d[:, c0:c0 + w_], ident[:, :])
            PE.post(ins, (src_res, r_const), (tr_,))
            sl = res("small_o")
            so = stage[0:w_, 0:128]
            copy_on(ev_eng(), so, tb[0:w_, 0:128], (tr_,), (r_stage,))
            dma(dst_rows_ap[c0:c0 + w_, :], so, reads=(r_stage,), writes=(sl,), is_out=True)
            c0 += w_

    def load_fm(dst_ap_2d, ncols, src_rows_ap, dst_res):
        c0 = 0
        while c0 < ncols:
            w_ = min(128, ncols - c0)
            dma(stage[0:w_, 0:128], src_rows_ap[c0:c0 + w_, :], reads=(), writes=(r_stage,))
            tb, tr_ = tr_bank()
            PE.pre((r_stage, r_const), (tr_,))
            ins = nc.tensor.transpose(tb[:, 0:w_], stage[0:w_, 0:128], ident[0:w_, 0:w_])
            PE.post(ins, (r_stage, r_const), (tr_,))
            copy_on(ev_eng(), dst_ap_2d[:, c0:c0 + w_], tb[:, 0:w_], (tr_,), (dst_res,))
            c0 += w_

    def run_tile(k, n, T0, src_ap, dst_ap):
        load_x(src_ap, n)
        for l in range(DEPTH):
            j = l // 2
            rmsnorm(n, l)
            fence()
            if l % 2 == 0:
                if "L" not in SKIP:
                    lru_layer(j, n, k)
            else:
                if "A" not in SKIP:
                    attn_layer(j, n, k, T0)
            rmsnorm(n, DEPTH + l)
            fence()
            if "F" not in SKIP:
                ffn_layer(l, n, k)
        store_y(dst_ap, n)

    def store_states(k, o_h, o_lc, o_fc):
        for j in range(NLRU):
            store_fm(hcar[k][:, j, :], DC, o_h[j], res("hcar%d" % k))
            for kk in range(3):
                store_fm(ucar[k][:, j, :, kk], DC, o_lc[j, kk], res("ucar%d" % k))
        for l in range(DEPTH):
            for r_ in range(2):
                store_fm(fcar[k][:, l, :, r_], 2 * FC, o_fc[l, r_], res("fcar%d" % k))

    def flat(ap):
        nd = len(ap.shape)
        if nd == 3:
            return ap
        if nd == 4:
            return ap.rearrange("a p c -> a p c") if False else ap.rearrange("a b p c -> (a b) p c")
        return ap.rearrange("a b d p c -> (a b d) p c")

    def make_cache(name, ap):
        fa = flat(ap)
        nt, _, ncols = fa.shape
        sc = nc.dram_tensor(name + "_bf", list(ap.shape), BF16, kind="Internal").ap()
        fs = flat(sc)
        for i in range(nt):
            precast(fa[i], fs[i], ncols)
        return sc

    w_in = [make_cache("w_in%d" % j, w_in[j]) for j in range(NLRU)]
    w_gate = make_cache("w_gate", w_gate)
    w_out = make_cache("w_out", w_out)
    w_qkv = [make_cache("w_qkv%d" % a, w_qkv[a]) for a in range(NATT)]
    w_o = make_cache("w_o", w_o)
    w_up = [make_cache("w_up%d" % l, w_up[l]) for l in range(DEPTH)]
    w_down = [make_cache("w_down%d" % l, w_down[l]) for l in range(DEPTH)]
    for ch in chans:
        q_wait(POOL, pool_waited, ch.last)

    for bp in range(NP if cfg.do_prompt else 0):
        for g in range(3):
            kvp[g] = kvp_f[g][bp]
        if bp > 0:
            X(DVE, lambda: nc.vector.memset(hcar[0][:], 0.0), (), (res("hcar0"),))
            X(DVE, lambda: nc.vector.memset(ucar[0][:], 0.0), (), (res("ucar0"),))
            X(DVE, lambda: nc.vector.memset(fcar[0][:], 0.0), (), (res("fcar0"),))
        for ti in range(NTILES):
            T0 = ti * NT
            run_tile(0, NT, T0, xp_f[bp, T0:T0 + NT], yp_f[bp, T0:T0 + NT])
        store_states(0, hp_f[bp], lcp_f[bp], fcp_f[bp])
    if cfg.do_sample:
        for bs in range(NSB):
            for g in range(3):
                caches[g] = caches_f[g][bs]
                kvs[g] = kvs_f[g][bs]
            for j in range(NLRU):
                load_fm(hcar[1][:, j, :], DC, st_h_f[bs, j], res("hcar1"))
                for kk in range(3):
                    load_fm(ucar[1][:, j, :, kk], DC, st_conv_f[bs, j, kk], res("ucar1"))
            for l in range(DEPTH):
                for r_ in range(2):
                    load_fm(fcar[1][:, l, :, r_], 2 * FC, st_ffn_f[bs, l, r_], res("fcar1"))
            run_tile(1, NS, 0, xs_f[bs], ys_f[bs])
            store_states(1, hs_f[bs], lcs_f[bs], fcs_f[bs])
    build.stats = dict(pe=PE.total, act=ACT.total, dve=DVE.total)
    for t in out_dma_toks:
        q_wait(SP, sp_waited, t)
    return nc


def _tile_w(W, kc_n=None):
    K, M = W.shape
    return np.ascontiguousarray(W.reshape(K // 128, 128, M // 128, 128).transpose(2, 1, 0, 3).reshape(M // 128, 128, (K // 128) * 128))


def _fm(v):
    s = v.shape
    return np.ascontiguousarray(np.moveaxis(v.reshape(s[:-1] + (s[-1] // 128, 128)), -1, 0))


def alibi_tables():
    n = 24
    slopes = (2.0 ** (-8.0 * np.arange(1, n + 1) / n)).astype(np.float32).reshape(3, NH)
    j = np.arange(128)[:, None]
    i = np.arange(128)[None, :]
    et = np.zeros((128, 3, NH, 2, 128), np.float32)
    for g, (win, d) in enumerate(GROUPS):
        for h in range(NH):
            sp = (i + 128 - j).astype(np.float32)
            et[:, g, h, 0, :] = np.where(i <= j, np.exp(-slopes[g, h] * (np.minimum(sp, 128.0) * d).astype(np.float32)), 0.0)
            sc = (i - j).astype(np.float32)
            et[:, g, h, 1, :] = np.where(i >= j, np.exp(-slopes[g, h] * (np.maximum(sc, 0.0) * d).astype(np.float32)), 0.0)
    e16 = np.zeros((128, NH, 128), np.float32)
    dj = (i - j)
    for h in range(NH):
        st = np.maximum(dj, 0) // 4
        e16[:, h, :] = np.where((dj >= 0) & (dj % 4 == 0), np.exp(-slopes[2, h] * (st * 16).astype(np.float32)), 0.0)
    return et, e16


def prep_shared(cfg, inp):
    D, DC, FC, G, NPASS, CH = cfg.D, cfg.DC, cfg.FC, cfg.G, cfg.NPASS, cfg.CH
    NLRU, NATT, DEPTH = cfg.NLRU, cfg.NATT, cfg.DEPTH
    f = lambda a: np.asarray(a, np.float32)
    sh = {}
    sh["nrm"] = _fm(np.concatenate([f(inp["norm_mix"]), f(inp["norm_ffn"]), f(inp["norm_final"])[None]], 0))
    b_in = f(inp["lru_b_in"])
    cw = f(inp["lru_conv_w"])
    ls = np.stack([b_in[:, :D], b_in[:, D:], cw[:, 0], cw[:, 1], cw[:, 2], cw[:, 3], f(inp["lru_conv_b"]),
                   f(inp["lru_b_a"]), f(inp["lru_b_i"]), f(inp["lru_lambda"])], 1)
    sh["lru_small"] = _fm(ls)
    sh["lru_bout"] = _fm(f(inp["lru_b_out"]))
    fw = f(inp["ffn_conv_w"])
    fs = np.stack([fw[:, 0], fw[:, 1], fw[:, 2], f(inp["ffn_conv_b"])], 1)
    sh["ffn_small"] = _fm(fs)
    et, e16 = alibi_tables()
    sh["etab"] = et
    sh["ecur16"] = e16
    sh["ident"] = np.eye(128, dtype=np.float32)
    for j in range(NLRU):
        sh["w_in%d" % j] = _tile_w(f(inp["lru_w_in"][j]))
    wg = np.zeros((NLRU, 2, DC, 128, CH * 128), np.float32)
    for j in range(NLRU):
        for t, nm in enumerate(("lru_w_a", "lru_w_i")):
            Wh = f(inp[nm][j])
            for hd in range(8):
                wg[j, t, hd * CH:(hd + 1) * CH] = _tile_w(Wh[hd])
    sh["w_gate"] = wg
    sh["w_out"] = np.stack([_tile_w(f(inp["lru_w_out"][j])) for j in range(NLRU)])
    for a in range(NATT):
        sh["w_qkv%d" % a] = _tile_w(f(inp["attn_w_qkv"][a]))
    sh["w_o"] = np.stack([_tile_w(f(inp["attn_w_o"][a])) for a in range(NATT)])
    wd = f(inp["ffn_w_down"])
    for l in range(DEPTH):
        sh["w_up%d" % l] = _tile_w(f(inp["ffn_w_up"][l]))
        sh["w_down%d" % l] = np.stack([_tile_w(wd[l, pi * G * 128:(pi + 1) * G * 128]) for pi in range(NPASS)])
    return sh


def prep_core(cfg, inp, bps, bss):
    f = lambda a: np.ascontiguousarray(np.asarray(a, np.float32))
    DC, FC = cfg.DC, cfg.FC
    m = {}
    m["xp"] = f(np.asarray(inp["x_prompt"])[bps])
    m["xs"] = f(np.asarray(inp["x_sample"])[bss])
    for g, nm in enumerate(("cache_kv_w128", "cache_kv_w512", "cache_kv_w2048")):
        m["cache%d" % g] = f(np.moveaxis(np.asarray(inp[nm])[:, bss], 1, 0))
    m["st_h"] = f(np.moveaxis(np.asarray(inp["state_lru_h"])[:, bss], 1, 0)).reshape(len(bss), cfg.NLRU, DC, 128)
    m["st_conv"] = f(np.moveaxis(np.asarray(inp["state_lru_conv"])[:, bss], 1, 0)).reshape(len(bss), cfg.NLRU, 3, DC, 128)
    m["st_ffn"] = f(np.moveaxis(np.asarray(inp["state_ffn_conv"])[:, bss], 1, 0)).reshape(len(bss), cfg.DEPTH, 2, 2 * FC, 128)
    return m


_NC_CACHE = {}


def kernel(**inputs):
    NPT, NST, NCORES = 2, 8, 8
    cfg = Cfg(NP=1, NSB=1)
    if "nc" not in _NC_CACHE:
        _NC_CACHE["nc"] = build(cfg)
    nc = _NC_CACHE["nc"]
    sh = prep_shared(cfg, inputs)
    in_maps = []
    for c in range(NCORES):
        m = dict(sh)
        m.update(prep_core(cfg, inputs, [c % NPT], [c]))
        in_maps.append(m)
    res = run_bass_kernel_spmd(nc, in_maps, core_ids=list(range(NCORES)))
    r = res.results
    D = cfg.D
    gp = lambda name, b: np.asarray(r[b][name])[0]
    gs = lambda name, b: np.asarray(r[b][name])[0]
    y_p = np.stack([gp("yp", b) for b in range(NPT)])
    y_s = np.stack([gs("ys", b) for b in range(NST)])
    kvp = [np.stack([gp("kvp%d" % g, b) for b in range(NPT)], 1) for g in range(3)]
    h_p = np.stack([gp("hp", b).reshape(cfg.NLRU, D) for b in range(NPT)], 1)
    lc_p = np.stack([gp("lcp", b).reshape(cfg.NLRU, 3, D) for b in range(NPT)], 1)
    fc_p = np.stack([gp("fcp", b).reshape(cfg.DEPTH, 2, 2 * cfg.DFF) for b in range(NPT)], 1)
    kvs = [np.stack([gs("kvs%d" % g, b) for b in range(NST)], 1) for g in range(3)]
    h_s = np.stack([gs("hs", b).reshape(cfg.NLRU, D) for b in range(NST)], 1)
    lc_s = np.stack([gs("lcs", b).reshape(cfg.NLRU, 3, D) for b in range(NST)], 1)
    fc_s = np.stack([gs("fcs", b).reshape(cfg.DEPTH, 2, 2 * cfg.DFF) for b in range(NST)], 1)
    outs = (y_p, y_s, kvp[0], kvp[1], kvp[2], h_p, lc_p, fc_p, kvs[0], kvs[1], kvs[2], h_s, lc_s, fc_s)
    return tuple(np.ascontiguousarray(o, dtype=np.float32) for o in outs)
```
